# Optimizing a Trainium2 kernel written in Bass

```python
import functools
import jax, jax.numpy as jnp
from jax import lax
import numpy as np

D_MODEL = 2048
BATCH = 16
SEQ = 256
DEPTH = 2
DEC_BATCH = 4
DEC_SEQ = 2048
PAST_LEN = 512

GRID_W = 64
N_MOD = 9
D_FF = 5504
FFN_RES = 0.5
EPS = 1e-6
NEG_INF = -1e30

POOL_GROUPS = 4
POOL_WINDOWS = (2, 4, 8, 16)
POOL_WIDTH = 1024
POOL_GC = POOL_WIDTH // POOL_GROUPS

NA_HEADS = 8
NA_HEAD_DIM = 128
NA_WIDTH = NA_HEADS * NA_HEAD_DIM
NA_WIN_H = 8
NA_WIN_W = 16
NA_QBLK = 16
NA_KBAND = NA_QBLK + NA_WIN_W
ATTN_QBLK = 128

GLA_HEADS = 4
GLA_DK = 128
GLA_DV = 256
GLA_KW = GLA_HEADS * GLA_DK
GLA_VW = GLA_HEADS * GLA_DV
GLA_RANK = 16
GLA_TAU = 16.0
GLA_CHUNK = 64
ROPE_BASE = 10000.0

BRANCH_W = 1024
N_BRANCH = 3
IN_SPLITS = (POOL_WIDTH, NA_WIDTH, NA_WIDTH, NA_WIDTH, GLA_KW, GLA_KW, GLA_VW, 2 * GLA_RANK, GLA_VW, N_BRANCH * D_MODEL)
IN_COLS = sum(IN_SPLITS)

kernel_name = 'hybrid_pool_na_gla_diffusion_step'

f32 = jnp.float32


def rmsnorm(x, g):
    xf = x.astype(f32)
    y = xf * lax.rsqrt(jnp.mean(xf * xf, axis=-1, keepdims=True) + EPS)
    return (y * g.astype(f32)).astype(x.dtype)


def modulate(x, shift, scale):
    return x * (1 + scale) + shift


def swiglu(h, w_in, w_out):
    gt, up = jnp.split(h @ w_in, 2, axis=-1)
    return (jax.nn.silu(gt) * up) @ w_out


def heads(x, n):
    B, L, _ = x.shape
    return x.reshape(B, L, n, -1).transpose(0, 2, 1, 3)


def merge_heads(x):
    B, H, L, d = x.shape
    return x.transpose(0, 2, 1, 3).reshape(B, L, H * d)


def split_in(h, w_in):
    idx = [int(i) for i in np.cumsum(IN_SPLITS)[:-1]]
    return jnp.split(h @ w_in, idx, axis=-1)


def pool_mix(u, w_grp, scale):
    B, L, _ = u.shape
    ug = u.reshape(B, L, POOL_GROUPS, POOL_GC)
    cs = jnp.concatenate([jnp.zeros((B, 1, POOL_GROUPS, POOL_GC), f32),
                          jnp.cumsum(ug.astype(f32), axis=1)], axis=1)
    t = jnp.arange(L)
    outs = []
    for gi, win in enumerate(POOL_WINDOWS):
        lo = jnp.clip(t - win // 2, 0, L - 1)
        hi = jnp.clip(t + win - 1 - win // 2, 0, L - 1)
        csg = cs[:, :, gi]
        cnt = (hi - lo + 1).astype(f32)[None, :, None]
        outs.append((csg[:, hi + 1] - csg[:, lo]) / cnt)
    pooled = jnp.stack(outs, axis=2).astype(u.dtype) - ug
    y = jnp.einsum('blgc,gcd->blgd', pooled, w_grp).reshape(B, L, POOL_WIDTH)
    return y * scale


def context_attention(q, k, v):
    B, H, L, d = q.shape
    nb = L // ATTN_QBLK
    qb = q.reshape(B, H, nb, ATTN_QBLK, d).transpose(2, 0, 1, 3, 4)

    def blk(qi):
        s = jnp.einsum('bhqd,bhkd->bhqk', qi, k).astype(f32) * (d ** -0.5)
        p = jax.nn.softmax(s, axis=-1).astype(v.dtype)
        return jnp.einsum('bhqk,bhkd->bhqd', p, v)

    o = lax.map(blk, qb)
    return o.transpose(1, 2, 0, 3, 4).reshape(B, H, L, d)


def na_latent(q, k, v, ck, cv, rpb):
    B, H, N, hd = q.shape
    rows = N // GRID_W
    kh = min(NA_WIN_H, rows)
    nb = GRID_W // NA_QBLK
    qc = np.arange(GRID_W).reshape(nb, NA_QBLK)
    cs = np.clip(qc - NA_WIN_W // 2, 0, GRID_W - NA_WIN_W)
    bs = np.clip(np.arange(nb) * NA_QBLK - NA_WIN_W // 2, 0, GRID_W - NA_KBAND)
    kc = bs[:, None] + np.arange(NA_KBAND)
    col_ok = (kc[:, None, :] >= cs[:, :, None]) & (kc[:, None, :] < cs[:, :, None] + NA_WIN_W)
    col_idx = np.clip(kc[:, None, :] - qc[:, :, None] + NA_WIN_W - 1, 0, 2 * NA_WIN_W - 2)
    scale = hd ** -0.5
    k_grid = k.reshape(B, H, rows, GRID_W, hd)
    v_grid = v.reshape(B, H, rows, GRID_W, hd)
    q_rows = jnp.moveaxis(q.reshape(B, H, rows, GRID_W, hd), 2, 0)
    n_loc = kh * NA_KBAND

    def row_block(args):
        qr, r = args
        rs = jnp.clip(r - kh // 2, 0, rows - kh)
        kb = lax.dynamic_slice_in_dim(k_grid, rs, kh, axis=2)[:, :, :, kc]
        vb = lax.dynamic_slice_in_dim(v_grid, rs, kh, axis=2)[:, :, :, kc]
        qb = qr.reshape(B, H, nb, NA_QBLK, hd)
        s_loc = jnp.einsum('bhnqd,bhinkd->bhnqik', qb, kb).astype(f32) * scale
        bias = rpb[:, rs + jnp.arange(kh) - r + NA_WIN_H - 1][:, :, col_idx]
        s_loc = jnp.where(col_ok[:, :, None, :], s_loc + bias.transpose(0, 2, 3, 1, 4).astype(f32), NEG_INF)
        s_ctx = jnp.einsum('bhnqd,bhcd->bhnqc', qb, ck).astype(f32) * scale
        s = jnp.concatenate([s_loc.reshape(B, H, nb, NA_QBLK, n_loc), s_ctx], axis=-1)
        p = jax.nn.softmax(s, axis=-1).astype(v.dtype)
        p_loc = p[..., :n_loc].reshape(B, H, nb, NA_QBLK, kh, NA_KBAND)
        o = (jnp.einsum('bhnqik,bhinkd->bhnqd', p_loc, vb)
             + jnp.einsum('bhnqc,bhcd->bhnqd', p[..., n_loc:], cv.astype(v.dtype)))
        return o.reshape(B, H, GRID_W, hd)

    o = lax.map(row_block, (q_rows, jnp.arange(rows)))
    return jnp.moveaxis(o, 0, 2).reshape(B, H, N, hd)


def axial_rope(x):
    L = x.shape[2]
    t = jnp.arange(L)
    pos = (t // GRID_W, t % GRID_W)
    half = GLA_DK // 2
    nf = half // 2
    inv = ROPE_BASE ** (-jnp.arange(nf, dtype=f32) / nf)
    parts = []
    for ax in range(2):
        xa = x[..., ax * half:(ax + 1) * half]
        ang = pos[ax].astype(f32)[:, None] * inv
        cos, sin = jnp.cos(ang), jnp.sin(ang)
        x1, x2 = xa[..., :nf], xa[..., nf:]
        parts += [x1 * cos - x2 * sin, x2 * cos + x1 * sin]
    return jnp.concatenate(parts, axis=-1)


def gla_chunked(q, k, v, g, s0):
    B, H, L, dk = q.shape
    dv = v.shape[-1]
    n = L // GLA_CHUNK
    rs = lambda a: a.reshape(B, H, n, GLA_CHUNK, a.shape[-1])
    q, k, v, g = rs(q), rs(k), rs(v), rs(g)
    b = jnp.cumsum(g, axis=3)
    b_end = b[:, :, :, -1:, :]
    q_in = q * jnp.exp(b)
    a = jnp.einsum('bhncd,bhnsd->bhncs', q_in, k * jnp.exp(-b))
    causal = jnp.tril(jnp.ones((GLA_CHUNK, GLA_CHUNK), dtype=bool))
    o_intra = jnp.einsum('bhncs,bhnse->bhnce', jnp.where(causal, a, 0.0), v)
    k_dec = k * jnp.exp(b_end - b)
    decay = jnp.exp(b_end[:, :, :, 0, :])

    def step(S, xs):
        qi, ki, vi, di = xs
        o = jnp.einsum('bhcd,bhde->bhce', qi, S)
        S = di[..., None] * S + jnp.einsum('bhcd,bhce->bhde', ki, vi)
        return S, o

    mv = lambda a: jnp.moveaxis(a, 2, 0)
    s_fin, o_inter = lax.scan(step, s0, (mv(q_in), mv(k_dec), mv(v), mv(decay)))
    o = o_intra + jnp.moveaxis(o_inter, 0, 2)
    return o.reshape(B, H, L, dv), s_fin


def gla_bidir(q, k, v, z, w_gate, b_gate, s0):
    outs, states = [], []
    for d in range(2):
        logit = jnp.einsum('blr,rk->blk', z[:, :, d], w_gate[d].astype(f32)) + b_gate[d].astype(f32)
        g = heads(jax.nn.log_sigmoid(logit) / GLA_TAU, GLA_HEADS)
        if d == 0:
            o, s = gla_chunked(q, k, v, g, s0[:, 0].astype(f32))
        else:
            fl = lambda a: jnp.flip(a, axis=2)
            o, s = gla_chunked(fl(q), fl(k), fl(v), fl(g), s0[:, 1].astype(f32))
            o = fl(o)
        outs.append(o)
        states.append(s)
    return outs[0] + outs[1], jnp.stack(states, axis=1)


def gla_out(o, norm_g, r):
    B, H, L, dv = o.shape
    o = o * lax.rsqrt(jnp.mean(o * o, axis=-1, keepdims=True) + EPS) * norm_g.astype(f32).reshape(H, 1, dv)
    return merge_heads(o).astype(r.dtype) * jax.nn.silu(r)


def merge_branches(y_pool, y_na, y_gla, gate_logits, w_branch, w_out):
    B, L, _ = y_pool.shape
    br = jnp.stack([y_pool, y_na.astype(y_pool.dtype), y_gla], axis=2)
    y = jnp.einsum('blnc,ncd->blnd', br, w_branch)
    g = jax.nn.sigmoid(gate_logits.reshape(B, L, N_BRANCH, D_MODEL).astype(f32)).astype(y.dtype)
    return jnp.sum(g * y, axis=2) @ w_out


def gla_qkv(gq, gk, gv, rope):
    q = heads(gq, GLA_HEADS).astype(f32)
    k = heads(gk, GLA_HEADS).astype(f32)
    if rope:
        q, k = axial_rope(q), axial_rope(k)
    return q * (GLA_DK ** -0.5), k, heads(gv, GLA_HEADS).astype(f32)


def context_mixer(h, w_in, pool_w, pool_scale, gla_w_gate, gla_b_gate, gla_norm, w_branch, w_out):
    B, L, _ = h.shape
    u, nq, nk, nv, gq, gk, gv, gz, gr, gl = split_in(h, w_in)
    y_pool = pool_mix(u, pool_w, pool_scale)
    nq, nk, nv = heads(nq, NA_HEADS), heads(nk, NA_HEADS), heads(nv, NA_HEADS)
    y_na = merge_heads(context_attention(nq, nk, nv))
    q, k, v = gla_qkv(gq, gk, gv, rope=False)
    s0 = jnp.zeros((B, 2, GLA_HEADS, GLA_DK, GLA_DV), f32)
    o, s_fin = gla_bidir(q, k, v, gz.reshape(B, L, 2, GLA_RANK).astype(f32), gla_w_gate, gla_b_gate, s0)
    y_gla = gla_out(o, gla_norm, gr)
    return merge_branches(y_pool, y_na, y_gla, gl, w_branch, w_out), (nk, nv, s_fin)


def latent_mixer(h, ck, cv, s_ctx, w_in, pool_w, pool_scale, na_rpb, gla_w_gate, gla_b_gate, gla_norm, w_branch, w_out):
    B, L, _ = h.shape
    u, nq, nk, nv, gq, gk, gv, gz, gr, gl = split_in(h, w_in)
    y_pool = pool_mix(u, pool_w, pool_scale)
    nq, nk, nv = heads(nq, NA_HEADS), heads(nk, NA_HEADS), heads(nv, NA_HEADS)
    y_na = merge_heads(na_latent(nq, nk, nv, ck, cv, na_rpb))
    q, k, v = gla_qkv(gq, gk, gv, rope=True)
    o, _ = gla_bidir(q, k, v, gz.reshape(B, L, 2, GLA_RANK).astype(f32), gla_w_gate, gla_b_gate, s_ctx)
    y_gla = gla_out(o, gla_norm, gr)
    return merge_branches(y_pool, y_na, y_gla, gl, w_branch, w_out), None


def trunk_layer(x, mod, pre, post, ffn_in, ffn_out, mixer_fn):
    m = lambda i: (mod[:, None, 3 * i], mod[:, None, 3 * i + 1], mod[:, None, 3 * i + 2])
    sh, sc, gt = m(0)
    y = swiglu(modulate(rmsnorm(x, pre[0]), sh, sc), ffn_in[0], ffn_out[0])
    x = x + FFN_RES * gt * rmsnorm(y, post[0])
    sh, sc, gt = m(1)
    y, aux = mixer_fn(modulate(rmsnorm(x, pre[1]), sh, sc))
    x = x + gt * rmsnorm(y, post[1])
    sh, sc, gt = m(2)
    y = swiglu(modulate(rmsnorm(x, pre[2]), sh, sc), ffn_in[1], ffn_out[1])
    x = x + FFN_RES * gt * rmsnorm(y, post[2])
    return x, aux


def setup_inputs(seed: int = 0) -> dict:
    key = jax.random.key(seed)
    ks = jax.random.split(key, 24)
    nrm = lambda i, shape, s=1.0: jax.random.normal(ks[i], shape, f32) * s
    D = D_MODEL
    return {
        'x_prompt': nrm(0, (BATCH, SEQ, D)),
        'x_sample': nrm(1, (DEC_BATCH, DEC_SEQ, D)),
        'c': nrm(2, (DEC_BATCH, D)),
        'cache_na_k': nrm(3, (DEC_BATCH, DEPTH, NA_HEADS, PAST_LEN, NA_HEAD_DIM)),
        'cache_na_v': nrm(4, (DEC_BATCH, DEPTH, NA_HEADS, PAST_LEN, NA_HEAD_DIM)),
        'state_gla': nrm(5, (DEC_BATCH, DEPTH, 2, GLA_HEADS, GLA_DK, GLA_DV)),
        'c_ctx': nrm(6, (D,)),
        'w_mod': nrm(7, (DEPTH, D, N_MOD * D), D ** -0.5),
        'b_mod': nrm(8, (DEPTH, N_MOD * D), 0.01),
        'norm_pre': 1.0 + nrm(9, (DEPTH, 3, D), 0.05),
        'norm_post': 1.0 + nrm(10, (DEPTH, 3, D), 0.05),
        'w_ffn_in': nrm(11, (DEPTH, 2, D, 2 * D_FF), D ** -0.5),
        'w_ffn_out': nrm(12, (DEPTH, 2, D_FF, D), D_FF ** -0.5),
        'w_in': nrm(13, (DEPTH, D, IN_COLS), D ** -0.5),
        'pool_w': nrm(14, (DEPTH, POOL_GROUPS, POOL_GC, POOL_GC), POOL_GC ** -0.5),
        'pool_scale': 1.0 + nrm(15, (DEPTH, POOL_WIDTH), 0.1),
        'na_rpb': nrm(16, (DEPTH, NA_HEADS, 2 * NA_WIN_H - 1, 2 * NA_WIN_W - 1), 0.1),
        'gla_w_gate': nrm(17, (DEPTH, 2, GLA_RANK, GLA_KW), GLA_RANK ** -0.5),
        'gla_b_gate': nrm(18, (DEPTH, 2, GLA_KW), 0.1),
        'gla_norm': 1.0 + nrm(19, (DEPTH, GLA_VW), 0.05),
        'w_branch': nrm(20, (DEPTH, N_BRANCH, BRANCH_W, D), BRANCH_W ** -0.5),
        'w_out': nrm(21, (DEPTH, D, D), D ** -0.5),
    }


def reference(x_prompt, x_sample, c, cache_na_k, cache_na_v, state_gla, c_ctx, w_mod, b_mod, norm_pre,
              norm_post, w_ffn_in, w_ffn_out, w_in, pool_w, pool_scale, na_rpb, gla_w_gate, gla_b_gate,
              gla_norm, w_branch, w_out):
    xp = x_prompt
    new_k, new_v, new_s = [], [], []
    for l in range(DEPTH):
        mod = (jax.nn.silu(c_ctx)[None] @ w_mod[l] + b_mod[l]).reshape(1, N_MOD, D_MODEL)
        mixer = functools.partial(context_mixer, w_in=w_in[l], pool_w=pool_w[l], pool_scale=pool_scale[l],
                                  gla_w_gate=gla_w_gate[l], gla_b_gate=gla_b_gate[l], gla_norm=gla_norm[l],
                                  w_branch=w_branch[l], w_out=w_out[l])
        xp, (k_l, v_l, s_l) = trunk_layer(xp, mod, norm_pre[l], norm_post[l], w_ffn_in[l], w_ffn_out[l], mixer)
        new_k.append(k_l)
        new_v.append(v_l)
        new_s.append(s_l)
    new_na_k = jnp.stack(new_k, axis=1)
    new_na_v = jnp.stack(new_v, axis=1)
    new_state_gla = jnp.stack(new_s, axis=1)

    xs = x_sample
    for l in range(DEPTH):
        mod = (jax.nn.silu(c) @ w_mod[l] + b_mod[l]).reshape(-1, N_MOD, D_MODEL)
        mixer = functools.partial(latent_mixer, ck=cache_na_k[:, l], cv=cache_na_v[:, l], s_ctx=state_gla[:, l],
                                  w_in=w_in[l], pool_w=pool_w[l], pool_scale=pool_scale[l], na_rpb=na_rpb[l],
                                  gla_w_gate=gla_w_gate[l], gla_b_gate=gla_b_gate[l], gla_norm=gla_norm[l],
                                  w_branch=w_branch[l], w_out=w_out[l])
        xs, _ = trunk_layer(xs, mod, norm_pre[l], norm_post[l], w_ffn_in[l], w_ffn_out[l], mixer)

    return (xp, xs, new_na_k, new_na_v, new_state_gla)
```

```python
import numpy as np
import ml_dtypes
from contextlib import ExitStack
import concourse.bass as bass
import concourse.mybir as mybir
from concourse.bass_utils import run_bass_kernel_spmd

F32 = mybir.dt.float32
BF16 = mybir.dt.bfloat16
AF = mybir.ActivationFunctionType
ALU = mybir.AluOpType
AX = mybir.AxisListType

D = 2048
NFC = 16
DFF = 5504
NJ = 43
DEPTH = 2
TP = 512
TS = 2048
T = TP + TS
NB = T // 512
NT = T // 128
EPS = 1e-6
INCOLS = 13344
C_U, C_NQ, C_NK, C_NV, C_GQ, C_GK, C_GV, C_GZ, C_GR, C_GL = 0, 1024, 2048, 3072, 4096, 4608, 5120, 6144, 6176, 7200
GRID_W = 64


class Sem:
    def __init__(self, K, name):
        self.sem = K.es.enter_context(K.nc.semaphore(name))
        self.count = 0
        self.name = name


class Buf:
    __slots__ = ("w", "r", "name")

    def __init__(self, name=""):
        self.w = None
        self.r = {}
        self.name = name


class Eng:
    def __init__(self, K, name, eng, same_sync=True):
        self.K = K
        self.name = name
        self.eng = eng
        self.src = Sem(K, "s_" + name)
        self.waited = {}
        self.same_sync = same_sync

    def wait(self, ev):
        src, val = ev
        if src is self.src and not self.same_sync:
            return
        if self.waited.get(src, 0) >= val:
            return
        self.eng.wait_ge(src.sem, val)
        self.waited[src] = val

    def sync(self, reads, writes):
        for b in reads:
            if b.w is not None:
                self.wait(b.w)
        for b in writes:
            if b.w is not None:
                self.wait(b.w)
            for s, v in b.r.items():
                self.wait((s, v))

    def done(self, ins, reads, writes):
        self.src.count += 1
        ins.then_inc(self.src.sem, 1)
        ev = (self.src, self.src.count)
        for b in reads:
            b.r[self.src] = ev[1]
        for b in writes:
            b.w = ev
            b.r = {}
        return ev

    def op(self, fn, reads=(), writes=()):
        self.sync(reads, writes)
        ins = fn(self.eng)
        return self.done(ins, reads, writes)


class DmaQ:
    def __init__(self, K, name, E, R=8):
        self.E = E
        self.slots = [Sem(K, "d_%s%d" % (name, i)) for i in range(R)]
        self.n = 0
        self.R = R

    def dma(self, out, in_, reads=(), writes=(), **kw):
        s = self.slots[self.n % self.R]
        self.n += 1
        if s.count > 0:
            self.E.wait((s, s.count))
        self.E.sync(reads, writes)
        ins = self.E.eng.dma_start(out=out, in_=in_, **kw)
        s.count += 16
        ins.then_inc(s.sem, 16)
        for b in reads:
            b.r[s] = s.count
        for b in writes:
            b.w = (s, s.count)
            b.r = {}
        return (s, s.count)


class Kctx:
    pass


def barrier(K):
    srcs = [e.src for e in K.engs] + [s for q in K.queues for s in q.slots]
    for e in K.engs:
        for s in srcs:
            if s is e.src or s.count == 0:
                continue
            e.wait((s, s.count))


_uid = [0]


def sb(K, st, name, shape, dt):
    _uid[0] += 1
    t = st.enter_context(K.nc.sbuf_tensor("%s_%d" % (name, _uid[0]), list(shape), dt))
    return t


DEFAULT_DBG = {}


def build(dbg=None):
    dbg = DEFAULT_DBG if dbg is None else dbg
    del _INPUT_NAMES[:]
    nc = bass.Bass("TRN2", target_bir_lowering=False)
    K = Kctx()
    K.nc = nc
    K.dbg = dbg or {}
    es = ExitStack()
    K.es = es
    dram = {}

    def din(name, shape, dt=F32):
        if name in dram:
            return dram[name]
        _INPUT_NAMES.append(name)
        dram[name] = nc.dram_tensor(name, list(shape), dt, kind="ExternalInput").ap()
        return dram[name]

    def dout(name, shape, dt=F32):
        dram[name] = nc.dram_tensor(name, list(shape), dt, kind="ExternalOutput").ap()
        return dram[name]

    def dint(name, shape, dt=F32):
        kind = "ExternalOutput" if name in K.dbg.get("expose", ()) else "Internal"
        dram[name] = nc.dram_tensor(name, list(shape), dt, kind=kind).ap()
        return dram[name]

    K.dram = dram
    K.din, K.dout, K.dint = din, dout, dint
    din("x_in", [T, D])
    dout("y_out", [T, D])
    dint("xT", [NFC, 128, T])
    dint("yT", [NFC, 128, T])
    dint("hidT", [NJ, 128, T], BF16)

    with es:
        K.pe = Eng(K, "pe", nc.tensor, same_sync=False)
        K.act = Eng(K, "act", nc.scalar)
        K.dve = Eng(K, "dve", nc.vector)
        K.pool = Eng(K, "pool", nc.gpsimd)
        K.sp = Eng(K, "sp", nc.sync)
        K.engs = [K.pe, K.act, K.dve, K.pool, K.sp]
        K.qs = DmaQ(K, "s", K.sp, R=8)
        K.qg = DmaQ(K, "g", K.pool, R=6)
        K.queues = [K.qs, K.qg]
        K.ps = []
        for i in range(8):
            t = es.enter_context(nc.psum_tensor("ps%d" % i, [128, 512], F32))
            K.ps.append((t, Buf("ps%d" % i)))
        K.ident = sb(K, es, "ident", [128, 128], F32)
        K.identb = sb(K, es, "identb", [128, 128], BF16)
        K.ones = sb(K, es, "ones", [128, 128], F32)
        K.cbuf = Buf("consts")
        din("ident_in", [128, 128])
        K.qs.dma(K.ident[:], dram["ident_in"], writes=[K.cbuf])
        K.dve.op(lambda e: e.memset(K.ones[:], 1.0), writes=[K.cbuf])
        K.epsc = sb(K, es, "epsc", [128, 1], F32)
        K.dve.op(lambda e: e.memset(K.epsc[:], EPS), writes=[K.cbuf])
        K.dve.op(lambda e: e.tensor_copy(out=K.identb[:], in_=K.ident[:]), reads=[K.cbuf], writes=[K.cbuf])
        K.onesb = sb(K, es, "onesbf", [128, 128], BF16)
        K.dve.op(lambda e: e.memset(K.onesb[:], 1.0), writes=[K.cbuf])
        K.rstd2 = sb(K, es, "rstd2", [128, T], F32)
        K.rstd2_buf = Buf("rstd2")

        K.big = sb(K, es, "big", [128, 44032], BF16)
        K.bigbuf = [Buf("big%d" % i) for i in range(NB)]
        K.wslots = [(sb(K, es, "wslot%d" % i, [128, 43, 128], BF16), Buf("w%d" % i)) for i in range(3)]
        K.wn = 0
        K.psi = 0
        K.abc = sb(K, es, "abc", [128, 2, 2, 3, 3, 16], F32)
        K.abcbuf = [Buf("abc0"), Buf("abc1")]

        stage_in(K)
        barrier(K)
        nl = K.dbg.get("layers", DEPTH)
        for l in range(nl):
            if l == 0:
                stage_mod(K, l)
                barrier(K)
            stage_norm(K, l, None, 0)
            barrier(K)
            for i in range(3):
                if i == 1:
                    if K.dbg.get("no_mixer"):
                        stage_norm(K, l, None, 2)
                        barrier(K)
                        continue
                    stage_mixer(K, l)
                else:
                    stage_ffn(K, l, i // 2, side_l=(l + 1 if (i == 2 and l + 1 < nl) else None))
                barrier(K)
                nxt = i + 1 if i < 2 else None
                stage_norm(K, l, i, nxt)
                barrier(K)
        stage_out(K)
        barrier(K)
    return nc


def ps_alloc(K):
    K.psi = (K.psi + 1) % 6
    return K.ps[K.psi]


def hT_view(K):
    return K.big[:, 0:NFC * T].rearrange("p (k t) -> p k t", k=NFC)


def gemm_b_gen(K, jobs, nk, nsb, ncol, rhs_fn, rhs_bufs, epi, mcols=128, slots=None, side=None, side_n=0):
    nc = K.nc
    nj = len(jobs)
    cnt_ = [0]

    depth = 2 if slots is None else len(slots) - 1

    def load(j):
        if slots is None:
            wt, wb = K.wslots[K.wn % 3]
            K.wn += 1
        else:
            wt, wb = slots[cnt_[0] % len(slots)]
            cnt_[0] += 1
        for i, piece in enumerate(jobs[j]):
            K.qg.dma(wt[:, i * nk:(i + 1) * nk, 0:mcols], piece.rearrange("(k p) n -> p k n", p=128), writes=[wb])
        return wt, wb

    pend = [load(j) for j in range(min(depth, nj))]
    deferred = None
    for j in range(nj):
        if j + depth < nj:
            pend.append(load(j + depth))
        wt, wb = pend.pop(0)
        if side is not None:
            for _ in range(side_n):
                next(side, None)
        yield
        for s in range(nsb):
            banks = [ps_alloc(K) for _ in jobs[j]]
            rb = [wb] + list(rhs_bufs(s))
            wbs = [b for _, b in banks]
            K.pe.sync(rb, wbs)
            for i in range(len(jobs[j])):
                for k in range(nk):
                    ins = nc.tensor.matmul(banks[i][0][0:mcols, 0:ncol], lhsT=wt[:, i * nk + k, 0:mcols], rhs=rhs_fn(k, s, i),
                                           start=(k == 0), stop=(k == nk - 1))
            K.pe.done(ins, rb, wbs)
            if deferred is not None:
                deferred()
            deferred = epi(j, s, banks)
    if deferred is not None:
        deferred()


def gemm_b(*a, **kw):
    for _ in gemm_b_gen(*a, **kw):
        pass


def load_rows_T(K, st, name, row_aps, out_ap_fn):
    nc = K.nc
    for i, ra in enumerate(row_aps):
        n = ra.shape[0]
        t = sb(K, st, "%s_r%d" % (name, i), [128, 128], F32)
        tb = Buf()
        K.qs.dma(t[0:n, :], ra, writes=[tb])
        pt, pb = ps_alloc(K)
        K.pe.sync([tb, K.cbuf], [pb])
        ins = nc.tensor.transpose(pt[:, 0:n], t[0:n, :], K.ident[0:n, 0:n])
        K.pe.done(ins, [tb, K.cbuf], [pb])
        out_ap_fn(i, pt, pb, n)


def mod_gen(K, l, st, mslots):
    nc = K.nc
    d = K.dram
    K.din("cvec", [2, D])
    K.din("w_mod", [DEPTH, D, 9 * D])
    K.din("b_mod", [DEPTH, 9 * D])
    K.din("norm_pre", [DEPTH, 3, D])
    K.din("norm_post", [DEPTH, 3, D])
    if True:
        scv = sb(K, st, "scv", [128, 32], BF16)
        scb = Buf()
        bmT = sb(K, st, "bmT", [128, 144], F32)
        bmb = Buf()
        gT = sb(K, st, "gT", [128, 96], F32)
        gb = Buf()
        modT = sb(K, st, "modT", [128, 2, 144], F32)
        mb = Buf()

        def o_sc(i, pt, pb, n):
            K.act.op(lambda e: e.activation(out=scv[:, 0:n], in_=pt[:, 0:n], func=AF.Silu), reads=[pb], writes=[scb])
        load_rows_T(K, st, "cv", [d["cvec"].rearrange("v (c p) -> (v c) p", p=128)], o_sc)

        def o_bm(i, pt, pb, n):
            K.dve.op(lambda e: e.tensor_copy(out=bmT[:, i * 128:i * 128 + n], in_=pt[:, 0:n]), reads=[pb], writes=[bmb])
        bm = d["b_mod"][l].rearrange("(c p) -> c p", p=128)
        load_rows_T(K, st, "bm", [bm[0:128, :], bm[128:144, :]], o_bm)

        def o_g(i, pt, pb, n):
            K.dve.op(lambda e: e.tensor_copy(out=gT[:, i * 48:i * 48 + n], in_=pt[:, 0:n]), reads=[pb], writes=[gb])
        load_rows_T(K, st, "gg", [d["norm_pre"][l].rearrange("i (c p) -> (i c) p", p=128),
                                  d["norm_post"][l].rearrange("i (c p) -> (i c) p", p=128)], o_g)
        scr = scv[:].rearrange("p (v c) -> p c v", v=2)
        jobs = [[d["w_mod"][l][:, j * 128:(j + 1) * 128]] for j in range(144)]

        def epi(j, s, banks):
            pt, pb = banks[0]
            K.dve.op(lambda e: e.tensor_scalar(out=modT[:, :, j], in0=pt[:, 0:2], scalar1=bmT[:, j:j + 1], scalar2=None, op0=ALU.add),
                     reads=[pb, bmb], writes=[mb])
        yield from gemm_b_gen(K, jobs, 16, 1, 2, lambda k, s, i: scr[:, k, :], lambda s: [scb], epi, slots=mslots)
        for v in range(2):
            for i in range(3):
                resw = 1.0 if i == 1 else 0.5
                K.dve.op(lambda e: e.scalar_tensor_tensor(out=K.abc[:, l % 2, v, i, 0, :], in0=modT[:, v, (3 * i + 1) * 16:(3 * i + 2) * 16], scalar=1.0,
                                                          in1=gT[:, i * 16:(i + 1) * 16], op0=ALU.add, op1=ALU.mult),
                         reads=[mb, gb], writes=[K.abcbuf[l % 2]])
                K.dve.op(lambda e: e.tensor_copy(out=K.abc[:, l % 2, v, i, 1, :], in_=modT[:, v, (3 * i) * 16:(3 * i + 1) * 16]),
                         reads=[mb], writes=[K.abcbuf[l % 2]])
                K.dve.op(lambda e: e.scalar_tensor_tensor(out=K.abc[:, l % 2, v, i, 2, :], in0=modT[:, v, (3 * i + 2) * 16:(3 * i + 3) * 16], scalar=resw,
                                                          in1=gT[:, 48 + i * 16:48 + (i + 1) * 16], op0=ALU.mult, op1=ALU.mult),
                         reads=[mb, gb], writes=[K.abcbuf[l % 2]])


def stage_mod(K, l):
    with ExitStack() as st:
        mslots = [(K.big[:, i * 2048:(i + 1) * 2048].rearrange("p (k n) -> p k n", k=16), Buf()) for i in range(12)]
        for _ in mod_gen(K, l, st, mslots):
            pass
        barrier(K)


def rstd_from_stats(K, out_ap, stat_ap, reads, writes, dim=D):
    K.act.op(lambda e: e.activation(out=out_ap, in_=stat_ap, func=AF.Sqrt, scale=1.0 / dim, bias=K.epsc[:, 0:1]), reads=list(reads) + [K.cbuf], writes=writes)
    K.dve.op(lambda e: e.reciprocal(out=out_ap, in_=out_ap), reads=writes, writes=writes)


def stage_norm(K, l, i_prev, i_next):
    nc = K.nc
    d = K.dram
    hT = hT_view(K)
    BS = 256
    with ExitStack() as st:
        xblks = [(sb(K, st, "xblk%d" % i, [128, NFC, BS], F32), Buf()) for i in range(2)]
        yblks = [(sb(K, st, "yblk%d" % i, [128, NFC, BS], F32), Buf()) for i in range(2)]
        rsb = [(sb(K, st, "rsb%d" % i, [128, BS], F32), Buf()) for i in range(2)]
        sqb16 = [(sb(K, st, "sqb%d" % i, [128, NFC, BS], BF16), Buf()) for i in range(1)]
        for b in range(T // BS):
            v = 0 if b < 2 else 1
            tsl = slice(b * BS, (b + 1) * BS)
            xt, xb = xblks[b % 2]
            yt, yb = yblks[b % 2]
            K.qs.dma(xt[:], d["xT"][:, :, tsl].rearrange("c p t -> p c t"), writes=[xb])
            if i_prev is not None:
                K.qs.dma(yt[:], d["yT"][:, :, tsl].rearrange("c p t -> p c t"), writes=[yb])
                K.dve.op(lambda e: e.tensor_tensor(out=yt[:], in0=yt[:], in1=K.rstd2[:, tsl].unsqueeze(1).to_broadcast([128, NFC, BS]), op=ALU.mult),
                         reads=[yb, K.rstd2_buf], writes=[yb])
                K.dve.op(lambda e: e.tensor_tensor(out=yt[:], in0=yt[:], in1=K.abc[:, l % 2, v, i_prev, 2, :].unsqueeze(2).to_broadcast([128, NFC, BS]), op=ALU.mult),
                         reads=[yb, K.abcbuf[l % 2]], writes=[yb])
                K.dve.op(lambda e: e.tensor_tensor(out=xt[:], in0=xt[:], in1=yt[:], op=ALU.add), reads=[yb, xb], writes=[xb])
                K.qs.dma(d["xT"][:, :, tsl].rearrange("c p t -> p c t"), xt[:], reads=[xb])
            if i_next is not None:
                stt, stb = K.ps[6 + b % 2]
                sq16, sq16b = sqb16[0]
                K.act.op(lambda e: e.activation(out=sq16[:], in_=xt[:], func=AF.Square), reads=[xb], writes=[sq16b])
                K.pe.sync([sq16b, K.cbuf], [stb])
                for fc in range(NFC):
                    ins = nc.tensor.matmul(stt[:, 0:BS], lhsT=K.onesb[:], rhs=sq16[:, fc, :], start=(fc == 0), stop=(fc == NFC - 1))
                K.pe.done(ins, [sq16b, K.cbuf], [stb])
                rt, rb = rsb[b % 2]
                rstd_from_stats(K, rt[:], stt[:, 0:BS], [stb], [rb])
                K.dve.op(lambda e: e.tensor_tensor(out=yt[:], in0=xt[:], in1=rt[:].unsqueeze(1).to_broadcast([128, NFC, BS]), op=ALU.mult),
                         reads=[xb, rb, yb], writes=[yb])
                for fc in range(NFC):
                    K.act.op(lambda e: e.activation(out=hT[:, fc, tsl], in_=yt[:, fc, :], func=AF.Identity, scale=K.abc[:, l % 2, v, i_next, 0, fc:fc + 1],
                                                    bias=K.abc[:, l % 2, v, i_next, 1, fc:fc + 1]), reads=[yb, K.abcbuf[l % 2]], writes=[K.bigbuf[b // 2]] if fc == NFC - 1 else [Buf()])


def stage_ffn(K, l, fi, side_l=None):
    nc = K.nc
    d = K.dram
    K.din("w_ffn_in", [DEPTH, 2, D, 2 * DFF])
    K.din("w_ffn_out", [DEPTH, 2, DFF, D])
    hT = hT_view(K)
    win = d["w_ffn_in"][l, fi]
    wout = d["w_ffn_out"][l, fi]
    with ExitStack() as st:
        sgs = [(sb(K, st, "sg%d" % i, [128, 512], F32), Buf()) for i in range(3)]
        hst = [(sb(K, st, "hst%d" % i, [128, T], BF16), Buf()) for i in range(2)]
        cnt = [0]
        jobs = [[win[:, j * 128:(j + 1) * 128], win[:, DFF + j * 128:DFF + (j + 1) * 128]] for j in range(NJ)]

        def epi(j, s, banks):
            (gt, gb), (ut, ub) = banks
            sg, sgb = sgs[cnt[0] % 3]
            cnt[0] += 1
            ht, hb = hst[j % 2]
            K.act.op(lambda e: e.activation(out=sg[:], in_=gt[:], func=AF.Silu), reads=[gb], writes=[sgb])
            K.dve.op(lambda e: e.tensor_tensor(out=ht[:, s * 512:(s + 1) * 512], in0=sg[:], in1=ut[:], op=ALU.mult), reads=[sgb, ub], writes=[hb])
            if s == NB - 1:
                K.qs.dma(d["hidT"][j], ht[:], reads=[hb])
        side = None
        if side_l is not None:
            mslots = [(sb(K, st, "mslot%d" % i, [128, 16, 128], BF16), Buf()) for i in range(9)]
            side = mod_gen(K, side_l, st, mslots)
        gemm_b(K, jobs, NFC, NB, 512, lambda k, s, i: hT[:, k, s * 512:(s + 1) * 512], lambda s: [K.bigbuf[s]], epi, side=side, side_n=4)
        if side is not None:
            for _ in side:
                pass
    barrier(K)
    gemm2(K, [[wout[:, fc * 128:(fc + 1) * 128]] for fc in range(NFC)], NJ, d["hidT"])


def gemm2(K, jobs, nk, src):
    with ExitStack() as st:
        tiles = gemm2_tiles(K, st)
        for (t0, blk) in ((0, 1024), (1024, 1024), (2048, 512)):
            hb_ = K.big[:, 0:nk * blk].rearrange("p (k t) -> p k t", k=nk)
            bufs = K.bigbuf[0:4]
            step = (nk + 3) // 4
            for q in range(4):
                k0, k1 = q * step, min(nk, (q + 1) * step)
                if k0 >= k1:
                    continue
                K.qs.dma(hb_[:, k0:k1, :], src[k0:k1, :, t0:t0 + blk].rearrange("k p t -> p k t"), writes=[bufs[q]])
            gemm2_core(K, tiles, jobs, nk, t0, blk, lambda k, s, i: hb_[:, k, s * 512:(s + 1) * 512], lambda s: bufs)
            barrier(K)


def gemm2_tiles(K, st):
    sqs = [(sb(K, st, "sq2_%d" % i, [128, 512], BF16), Buf()) for i in range(3)]
    yst = [(sb(K, st, "yst%d" % i, [128, 1024], F32), Buf()) for i in range(2)]
    return sqs, yst, [0]


def gemm2_core(K, tiles, jobs, nk, t0, blk, rhs_fn, rhs_bufs):
    nc = K.nc
    d = K.dram
    sqs, yst, cnt = tiles
    nsb = blk // 512
    nj = len(jobs)

    def epi(j, s, banks):
        pt, pb = banks[0]
        yt, yb = yst[j % 2]
        sq, sqb = sqs[cnt[0] % 3]
        cnt[0] += 1
        K.act.op(lambda e: e.copy(out=yt[:, s * 512:(s + 1) * 512], in_=pt[:]), reads=[pb], writes=[yb])
        K.act.op(lambda e: e.activation(out=sq[:], in_=pt[:], func=AF.Square), reads=[pb], writes=[sqb])
        if s == nsb - 1:
            K.qs.dma(d["yT"][j][:, t0:t0 + blk], yt[:, 0:blk], reads=[yb])
        stt, stb = K.ps[6 + s]

        def dfr():
            K.pe.sync([sqb, K.cbuf], [stb] if j == 0 else [])
            ins = nc.tensor.matmul(stt[:], lhsT=K.onesb[:], rhs=sq[:], start=(j == 0), stop=(j == nj - 1))
            K.pe.done(ins, [sqb, K.cbuf], [stb])
        return dfr
    gemm_b(K, jobs, nk, nsb, 512, rhs_fn, rhs_bufs, epi)
    for s in range(nsb):
        stt, stb = K.ps[6 + s]
        rstd_from_stats(K, K.rstd2[:, t0 + s * 512:t0 + (s + 1) * 512], stt[:], [stb], [K.rstd2_buf])


SEQS = ((0, 256, False), (256, 256, True), (512, 2048, True))
SEQS = ((0, 256, 0), (256, 256, 0), (512, 2048, 1))
SCALE = 128.0 ** -0.5


def stage_mixer(K, l):
    d = K.dram
    for nm, shp, dt in (("naqT", [8, 128, T], BF16), ("nakT", [8, 128, T], BF16), ("u_tok", [T, 1024], BF16),
                        ("nav_tok", [T, 1024], BF16), ("gv_tok", [T, 1024], BF16), ("gqT", [4, 128, T], F32),
                        ("gqpT", [4, 128, T], F32), ("gkT", [4, 128, T], F32), ("gkpT", [4, 128, T], F32),
                        ("gzT", [2, 16, T], F32), ("srT", [8, 128, T], F32), ("sigT", [48, 128, T], F32),
                        ("brT", [24, 128, T], BF16), ("ogT", [8, 128, T], F32)):
        if nm not in d:
            K.dint(nm, shp, dt)
    K.din("w_in", [DEPTH, D, INCOLS])
    K.din("w_rope", [DEPTH, D, 1024])
    if "new_k" not in d:
        K.dout("new_k", [2, DEPTH, 8, 256, 128])
        K.dout("new_v", [2, DEPTH, 8, 256, 128])
        K.dout("new_s", [2, DEPTH, 2, 4, 128, 256])
    mix_inproj(K, l)
    barrier(K)
    if K.dbg.get("mix_upto", 9) >= 2:
        mix_pool(K, l)
        barrier(K)
    if K.dbg.get("mix_upto", 9) >= 3:
        mix_na(K, l)
        barrier(K)
    if K.dbg.get("mix_upto", 9) >= 4:
        mix_gla(K, l)
        barrier(K)
    if K.dbg.get("mix_upto", 9) >= 5:
        mix_merge(K, l)
        barrier(K)


def mix_inproj(K, l):
    nc = K.nc
    d = K.dram
    hT = hT_view(K)
    win = d["w_in"][l]
    wr = d["w_rope"][l]
    with ExitStack() as st:
        stf = [(sb(K, st, "stf%d" % i, [128, T], F32), Buf()) for i in range(2)]
        stb16 = [(sb(K, st, "stb%d" % i, [128, T], BF16), Buf()) for i in range(2)]
        spec = []
        for h in range(8):
            spec.append((win[:, C_NQ + h * 128:C_NQ + (h + 1) * 128], None, d["naqT"][h], BF16))
        for h in range(8):
            spec.append((win[:, C_NK + h * 128:C_NK + (h + 1) * 128], None, d["nakT"][h], BF16))
        for h in range(4):
            spec.append((win[:, C_GQ + h * 128:C_GQ + (h + 1) * 128], None, d["gqT"][h], F32))
            spec.append((win[:, C_GK + h * 128:C_GK + (h + 1) * 128], None, d["gkT"][h], F32))
            spec.append((wr[:, h * 128:(h + 1) * 128], None, d["gqpT"][h], F32))
            spec.append((wr[:, 512 + h * 128:512 + (h + 1) * 128], None, d["gkpT"][h], F32))
        for c in range(8):
            spec.append((win[:, C_GR + c * 128:C_GR + (c + 1) * 128], AF.Silu, d["srT"][c], F32))
        for c in range(48):
            spec.append((win[:, C_GL + c * 128:C_GL + (c + 1) * 128], AF.Sigmoid, d["sigT"][c], F32))
        cnt = {F32: 0, BF16: 0}

        def epi(j, s, banks):
            pt, pb = banks[0]
            _, func, dst, dt = spec[j]
            pool_ = stf if dt == F32 else stb16
            if s == 0:
                cnt[dt] += 1
            stt, stbuf = pool_[cnt[dt] % 2]
            o = stt[:, s * 512:(s + 1) * 512]
            if func is None:
                K.dve.op(lambda e: e.tensor_copy(out=o, in_=pt[:]), reads=[pb], writes=[stbuf])
            else:
                K.act.op(lambda e: e.activation(out=o, in_=pt[:], func=func), reads=[pb], writes=[stbuf])
            if s == NB - 1:
                K.qs.dma(dst, stt[:], reads=[stbuf])
        parts = K.dbg.get('parts', 'ABCD')
        if 'A' in parts:
          gemm_b(K, [[sp[0]] for sp in spec], NFC, NB, 512, lambda k, s, i: hT[:, k, s * 512:(s + 1) * 512], lambda s: [K.bigbuf[s]], epi)

        def epi_z(j, s, banks):
            pt, pb = banks[0]
            if s == 0:
                cnt[F32] += 1
            stt, stbuf = stf[cnt[F32] % 2]
            K.dve.op(lambda e: e.tensor_copy(out=stt[0:16, s * 512:(s + 1) * 512], in_=pt[0:16, :]), reads=[pb], writes=[stbuf])
            if s == NB - 1:
                K.qs.dma(d["gzT"][j], stt[0:16, :], reads=[stbuf])
        if 'B' in parts:
          gemm_b(K, [[win[:, C_GZ + dd * 16:C_GZ + (dd + 1) * 16]] for dd in range(2)], NFC, NB, 512,
               lambda k, s, i: hT[:, k, s * 512:(s + 1) * 512], lambda s: [K.bigbuf[s]], epi_z, mcols=16)

        slabs = [(sb(K, st, "slab%d" % i, [128, NFC, 512], BF16), Buf()) for i in range(2)]
        tst = [(sb(K, st, "tst%d" % i, [128, 512], BF16), Buf()) for i in range(3)]
        tsf = [(sb(K, st, "tsf%d" % i, [128, 512], F32), Buf()) for i in range(2)]
        jobs = []
        for half in range(2):
            jobs.append((C_U + half * 512, d["u_tok"], half, None, NT))
        for half in range(2):
            jobs.append((C_NV + half * 512, d["nav_tok"], half, "new_v", NT))
        for half in range(2):
            jobs.append((C_GV + half * 512, d["gv_tok"], half, None, NT))
        for half in range(2):
            jobs.append((C_NK + half * 512, None, half, "new_k", 4))
        n3 = 0
        n2 = 0
        if 'C' not in parts:
            jobs = []
        for ji, (c0, dst, half, outname, ntl) in enumerate(jobs):
            sl, slb = slabs[ji % 2]
            for q4 in range(4):
                K.qg.dma(sl[:, :, q4 * 128:(q4 + 1) * 128], win[:, c0 + q4 * 128:c0 + (q4 + 1) * 128].rearrange("(k p) n -> p k n", p=128), writes=[slb])
            for t in range(ntl):
                pt, pb = ps_alloc(K)
                rb = [slb, K.bigbuf[t // 4]]
                K.pe.sync(rb, [pb])
                for k in range(NFC):
                    ins = nc.tensor.matmul(pt[:], lhsT=hT[:, k, t * 128:(t + 1) * 128], rhs=sl[:, k, :], start=(k == 0), stop=(k == NFC - 1))
                K.pe.done(ins, rb, [pb])
                if dst is not None:
                    tt_, tb = tst[n3 % 3]
                    n3 += 1
                    K.act.op(lambda e: e.activation(out=tt_[:], in_=pt[:], func=AF.Copy), reads=[pb], writes=[tb])
                    K.qs.dma(dst[t * 128:(t + 1) * 128, half * 512:(half + 1) * 512], tt_[:], reads=[tb])
                if outname is not None and t < 4 and 'D' in parts:
                    tf, tfb = tsf[n2 % 2]
                    n2 += 1
                    K.act.op(lambda e: e.activation(out=tf[:], in_=pt[:], func=AF.Copy), reads=[pb], writes=[tfb])
                    K.qs.dma(d[outname][t // 2, l, half * 4:(half + 1) * 4, (t % 2) * 128:(t % 2 + 1) * 128, :].rearrange("h p e -> p h e"),
                             tf[:].rearrange("p (h e) -> p h e", h=4), reads=[tfb])


def mix_pool(K, l):
    nc = K.nc
    d = K.dram
    K.din("pool_pm", [128, 20, 128], BF16)
    K.din("pool_w", [DEPTH, 4, 256, 256])
    K.din("pool_scale", [DEPTH, 1024])
    with ExitStack() as st:
        pm = sb(K, st, "pm", [128, 20, 128], BF16)
        pw = sb(K, st, "pw", [128, 4, 2, 256], BF16)
        psc = sb(K, st, "psc", [128, 8], F32)
        cb = Buf()
        K.qs.dma(pm[:], d["pool_pm"], writes=[cb])
        K.qg.dma(pw[:], d["pool_w"][l].rearrange("g (cc p) e -> p g cc e", p=128), writes=[cb])

        def o_ps(i, pt, pb, n):
            K.dve.op(lambda e: e.tensor_copy(out=psc[:, 0:n], in_=pt[:, 0:n]), reads=[pb], writes=[cb])
        load_rows_T(K, st, "psc", [d["pool_scale"][l].rearrange("(c p) -> c p", p=128)], o_ps)
        ut = sb(K, st, "ut", [128, 16, 1024], BF16)
        ub = Buf()
        pooled = [(sb(K, st, "pooled%d" % i, [128, 2, 512], BF16), Buf()) for i in range(2)]
        yst = [(sb(K, st, "pyst%d" % i, [128, 512], BF16), Buf()) for i in range(3)]
        n = 0
        ny = 0
        for (t0, L, smp) in SEQS:
            ntl = L // 128
            K.qs.dma(ut[:, 0:ntl, :], d["u_tok"][t0:t0 + L, :].rearrange("(j p) c -> p j c", p=128), writes=[ub])
            gsz = min(4, ntl)
            for g0 in range(0, ntl, gsz):
                ncol = gsz * 128
                for g in range(4):
                    pl, plb = pooled[n % 2]
                    n += 1
                    for cc in range(2):
                        pt, pb = ps_alloc(K)
                        K.pe.sync([ub, cb], [pb])
                        for jl in range(gsz):
                            j = g0 + jl
                            contrib = []
                            if j > 0:
                                contrib.append((j - 1, 3))
                            contrib.append((j, 1 if j == 0 else (2 if j == ntl - 1 else 0)))
                            if j < ntl - 1:
                                contrib.append((j + 1, 4))
                            for ci, (jp, var) in enumerate(contrib):
                                ins = nc.tensor.matmul(pt[:, jl * 128:(jl + 1) * 128], lhsT=ut[:, jp, g * 256 + cc * 128:g * 256 + (cc + 1) * 128],
                                                       rhs=pm[:, g * 5 + var, :], start=(ci == 0), stop=(ci == len(contrib) - 1))
                        K.pe.done(ins, [ub, cb], [pb])
                        K.dve.op(lambda e: e.tensor_copy(out=pl[:, cc, 0:ncol], in_=pt[:, 0:ncol]), reads=[pb], writes=[plb])
                    for dc in range(2):
                        pt, pb = ps_alloc(K)
                        K.pe.sync([plb, cb], [pb])
                        for cc in range(2):
                            ins = nc.tensor.matmul(pt[:, 0:ncol], lhsT=pw[:, g, cc, dc * 128:(dc + 1) * 128], rhs=pl[:, cc, 0:ncol], start=(cc == 0), stop=(cc == 1))
                        K.pe.done(ins, [plb, cb], [pb])
                        ys, ysb = yst[ny % 3]
                        ny += 1
                        K.act.op(lambda e: e.activation(out=ys[:, 0:ncol], in_=pt[:, 0:ncol], func=AF.Copy, scale=psc[:, g * 2 + dc:g * 2 + dc + 1]), reads=[pb, cb], writes=[ysb])
                        K.qs.dma(d["brT"][g * 2 + dc][:, t0 + g0 * 128:t0 + g0 * 128 + ncol], ys[:, 0:ncol], reads=[ysb])


def mix_na(K, l):
    nc = K.nc
    d = K.dram
    K.din("na_tt", [DEPTH, 64, 8, 15, 64])
    K.din("ck_in", [DEPTH, 8, 512, 128])
    K.din("cv_in", [DEPTH, 8, 512, 128])
    qv = K.big[:, 0:8192].rearrange("p (h t) -> p h t", h=4)
    kv = K.big[:, 8192:16384].rearrange("p (h t) -> p h t", h=4)
    vv = K.big[:, 16384:24576].rearrange("p (j c) -> p j c", j=16)
    qkb = K.bigbuf[0]
    with ExitStack() as st:
        tt = sb(K, st, "natt", [64, 4, 15, 64], F32)
        ckT = sb(K, st, "ckT", [128, 4, 512], BF16)
        cvt = sb(K, st, "cvt", [128, 4, 4, 128], BF16)
        ckf = sb(K, st, "ckf", [128, 4, 4, 128], F32)
        cxb = Buf()
        onesb = sb(K, st, "onesb", [128, 128], BF16)
        K.dve.op(lambda e: e.memset(onesb[:], 1.0), writes=[cxb])
        pts = [(sb(K, st, "napt%d" % i, [64, 1152], BF16), Buf()) for i in range(4)]
        sls = [(sb(K, st, "nasl%d" % i, [64, 512], F32), Buf()) for i in range(4)]
        ptt = [(sb(K, st, "naptt%d" % i, [128, 576], BF16), Buf()) for i in range(4)]
        rec = sb(K, st, "narec", [128, 512], F32)
        recb = Buf()
        yst = [(sb(K, st, "nayst%d" % i, [128, 512], BF16), Buf()) for i in range(2)]
        for pt_, pb_ in pts:
            K.pool.op(lambda e: e.memset(pt_[:], 0.0), writes=[pb_])
        nu = 0
        ng = 0
        for hh in range(2):
            for (t0, L, smp) in SEQS:
                ntl = L // 128
                K.qs.dma(qv[:, :, 0:L], d["naqT"][hh * 4:(hh + 1) * 4, :, t0:t0 + L].rearrange("h p t -> p h t"), writes=[qkb])
                K.qs.dma(kv[:, :, 0:L], d["nakT"][hh * 4:(hh + 1) * 4, :, t0:t0 + L].rearrange("h p t -> p h t"), writes=[qkb])
                K.qs.dma(vv[:, 0:ntl, :], d["nav_tok"][t0:t0 + L, hh * 512:(hh + 1) * 512].rearrange("(j p) c -> p j c", p=128), writes=[qkb])
                if smp:
                    K.qs.dma(tt[:], d["na_tt"][l][:, hh * 4:(hh + 1) * 4, :, :], writes=[cxb])
                    for c4 in range(4):
                        K.qg.dma(cvt[:, c4, :, :], d["cv_in"][l][hh * 4:(hh + 1) * 4, c4 * 128:(c4 + 1) * 128, :].rearrange("h p e -> p h e"), writes=[cxb])
                        K.qs.dma(ckf[:, c4, :, :], d["ck_in"][l][hh * 4:(hh + 1) * 4, c4 * 128:(c4 + 1) * 128, :].rearrange("h p e -> p h e"), writes=[cxb])
                    for h in range(4):
                        pt, pb = ps_alloc(K)
                        K.pe.sync([cxb, K.cbuf], [pb])
                        for c in range(4):
                            ins = nc.tensor.transpose(pt[:, c * 128:(c + 1) * 128], ckf[:, c, h, :], K.ident[:])
                        K.pe.done(ins, [cxb, K.cbuf], [pb])
                        K.act.op(lambda e: e.activation(out=ckT[:, h, :], in_=pt[:], func=AF.Copy), reads=[pb], writes=[cxb])
                nrow = L // 64
                gsz = 8 if smp else 4
                units = []
                for h in range(4):
                    for r0 in range(0, nrow, gsz):
                        for jr in range(gsz):
                            units.append((h, r0, jr))
                state = {}

                def ph1(ui):
                    h, r0, jr = units[ui]
                    r = r0 + jr
                    if smp:
                        rs = min(max(r - 4, 0), 24)
                        nloc, a0, off, nch, k0 = 512, rs // 2, 64 * (rs % 2), 4 + (rs % 2), rs * 64
                    else:
                        rs, nloc, a0, off, nch, k0 = 0, 256, 0, 0, 2, 0
                    pt_, ptb = pts[ui % 4]
                    sl_, slb = sls[ui % 4]
                    state[ui] = (a0, nch)
                    q_ap = qv[:, h, r * 64:(r + 1) * 64]
                    ba, bab = ps_alloc(K)
                    K.pe.sync([qkb], [bab])
                    ins = nc.tensor.matmul(ba[0:64, 0:nloc], lhsT=q_ap, rhs=kv[:, h, k0:k0 + nloc], start=True, stop=True)
                    K.pe.done(ins, [qkb], [bab])
                    if smp:
                        bc, bcb = ps_alloc(K)
                        K.pe.sync([qkb, cxb], [bcb])
                        ins = nc.tensor.matmul(bc[0:64, 0:512], lhsT=q_ap, rhs=ckT[:, h, :], start=True, stop=True)
                        K.pe.done(ins, [qkb, cxb], [bcb])
                        dr0 = rs - r + 7
                        K.dve.op(lambda e: e.scalar_tensor_tensor(out=sl_[:].rearrange("p (i k) -> p i k", i=8), in0=ba[0:64, 0:512].rearrange("p (i k) -> p i k", i=8),
                                                                  scalar=SCALE, in1=tt[:, h, dr0:dr0 + 8, :], op0=ALU.mult, op1=ALU.add),
                                 reads=[bab, cxb], writes=[slb])
                        if off:
                            K.pool.op(lambda e: e.memset(pt_[:, 0:64], 0.0), writes=[ptb])
                            K.pool.op(lambda e: e.memset(pt_[:, 576:640], 0.0), writes=[ptb])
                        K.act.op(lambda e: e.activation(out=pt_[:, off:off + 512], in_=sl_[:], func=AF.Exp), reads=[slb], writes=[ptb])
                        K.act.op(lambda e: e.activation(out=pt_[:, 640:1152], in_=bc[0:64, 0:512], func=AF.Exp, scale=SCALE), reads=[bcb], writes=[ptb])
                    else:
                        K.act.op(lambda e: e.activation(out=pt_[:, 0:256], in_=ba[0:64, 0:256], func=AF.Exp, scale=SCALE), reads=[bab], writes=[ptb])

                def ph2(ui):
                    a0, nch = state[ui]
                    pt_, ptb = pts[ui % 4]
                    ptt_, pttb = ptt[ui % 4]
                    chunks = [(c * 128, c * 64) for c in range(nch)]
                    if smp:
                        chunks += [(640 + c * 128, 320 + c * 64) for c in range(4)]
                    tl, tlb = ps_alloc(K)
                    tl16 = tl.bitcast(BF16)
                    K.pe.sync([ptb, K.cbuf], [tlb])
                    for (pc, oc) in chunks:
                        ins = nc.tensor.transpose(tl16[:, oc:oc + 64], pt_[:, pc:pc + 128], K.identb[0:64, 0:64])
                    K.pe.done(ins, [ptb, K.cbuf], [tlb])
                    ncp = 576 if smp else nch * 64
                    K.act.op(lambda e: e.activation(out=ptt_[:, 0:ncp], in_=tl16[:, 0:ncp], func=AF.Copy), reads=[tlb], writes=[pttb])

                def ph3(ui):
                    h, r0, jr = units[ui]
                    a0, nch = state.pop(ui)
                    ptt_, pttb = ptt[ui % 4]
                    numt, numb = K.ps[6]
                    dent, denb = K.ps[7]
                    ops = [(vv[:, a0 + c, h * 128:(h + 1) * 128], ptt_[:, c * 64:(c + 1) * 64]) for c in range(nch)]
                    if smp:
                        ops += [(cvt[:, c, h, :], ptt_[:, 320 + c * 64:320 + (c + 1) * 64]) for c in range(4)]
                    first = (jr == 0)
                    K.pe.sync([pttb, qkb, cxb], [numb, denb] if first else [])
                    for oi, (va, pa) in enumerate(ops):
                        nc.tensor.matmul(numt[:, jr * 64:(jr + 1) * 64], lhsT=va, rhs=pa, start=(oi == 0), stop=(oi == len(ops) - 1))
                    for oi, (va, pa) in enumerate(ops):
                        ins = nc.tensor.matmul(dent[:, jr * 64:(jr + 1) * 64], lhsT=onesb[:], rhs=pa, start=(oi == 0), stop=(oi == len(ops) - 1))
                    K.pe.done(ins, [pttb, qkb, cxb], [numb, denb])
                    if jr == gsz - 1:
                        ncol = gsz * 64
                        K.dve.op(lambda e: e.reciprocal(out=rec[:, 0:ncol], in_=dent[:, 0:ncol]), reads=[denb], writes=[recb])
                        ys, ysb = yst[(ui // gsz) % 2]
                        K.dve.op(lambda e: e.tensor_tensor(out=ys[:, 0:ncol], in0=numt[:, 0:ncol], in1=rec[:, 0:ncol], op=ALU.mult), reads=[numb, recb], writes=[ysb])
                        K.qs.dma(d["brT"][8 + hh * 4 + h][:, t0 + r0 * 64:t0 + r0 * 64 + ncol], ys[:, 0:ncol], reads=[ysb])

                nun = len(units)
                for i in range(nun + 2):
                    if i < nun:
                        ph1(i)
                    if 0 <= i - 1 < nun:
                        ph2(i - 1)
                    if 0 <= i - 2 < nun:
                        ph3(i - 2)


def mix_gla(K, l):
    nc = K.nc
    d = K.dram
    K.din("gla_w_gate", [DEPTH, 2, 16, 512])
    K.din("gla_b_gate", [DEPTH, 2, 512])
    K.din("gla_norm", [DEPTH, 1024])
    K.din("gla_tri", [128, 2, 256])
    K.din("gla_mask", [128, 2, 128])
    K.din("rope_cos", [128, TS])
    K.din("rope_sin", [128, TS])
    K.din("sg_in", [DEPTH, 2, 4, 128, 256])
    qin = K.big[:, 0:8192].rearrange("p (h t) -> p h t", h=4)
    kout = K.big[:, 8192:16384].rearrange("p (h t) -> p h t", h=4)
    kdt = K.big[:, 16384:24576].rearrange("p (j h e) -> p j h e", j=16, h=4)
    vv = K.big[:, 24576:40960].rearrange("p (j c) -> p j c", j=16)
    vb, qb_, kb_, kdb = K.bigbuf[0], K.bigbuf[1], K.bigbuf[2], K.bigbuf[3]
    with ExitStack() as st:
        cb = Buf()
        wg = sb(K, st, "wg", [16, 2, 512], F32)
        bgb = sb(K, st, "bgb", [128, 2, 512], F32)
        tri = sb(K, st, "tri", [128, 2, 256], F32)
        msk = sb(K, st, "gmsk", [128, 2, 128], F32)
        gn = sb(K, st, "gn", [128, 8], F32)
        css = [(sb(K, st, "rcs%d" % i, [128, 2, 128], F32), Buf()) for i in range(2)]
        K.qs.dma(wg[:], d["gla_w_gate"][l].rearrange("d r k -> r d k"), writes=[cb])
        K.qs.dma(bgb[:].rearrange("p d k -> p (d k)"), d["gla_b_gate"][l].rearrange("d k -> (d k)").partition_broadcast(128), writes=[cb])
        K.qs.dma(tri[:], d["gla_tri"], writes=[cb])
        K.qs.dma(msk[:], d["gla_mask"], writes=[cb])

        def o_gn(i, pt, pb, n):
            K.dve.op(lambda e: e.tensor_copy(out=gn[:, 0:n], in_=pt[:, 0:n]), reads=[pb], writes=[cb])
        load_rows_T(K, st, "gn", [d["gla_norm"][l].rearrange("(c p) -> c p", p=128)], o_gn)
        zss = [(sb(K, st, "zs%d" % i, [16, 2, 128], F32), Buf()) for i in range(2)]
        lgs = [(sb(K, st, "lg%d" % i, [128, 512], F32), Buf()) for i in range(2)]
        qfs = [(sb(K, st, "qf%d" % i, [128, 4, 4, 128], F32), Buf()) for i in range(2)]
        e1s = [(sb(K, st, "e1_%d" % i, [128, 256], F32), Buf()) for i in range(2)]
        e2s = [(sb(K, st, "e2_%d" % i, [128, 128], F32), Buf()) for i in range(2)]
        rts = [(sb(K, st, "rt%d" % i, [128, 4, 128], F32), Buf()) for i in range(2)]
        kds = [(sb(K, st, "kd%d" % i, [128, 4, 128], F32), Buf()) for i in range(2)]
        dec = sb(K, st, "dec", [128, 4, 32], F32)
        decb = Buf()
        Sf = sb(K, st, "Sf", [128, 4, 256], F32)
        Sb = sb(K, st, "Sb", [128, 4, 256], BF16)
        Sfb = [Buf() for _ in range(4)]
        Sbb = [Buf() for _ in range(4)]
        atm = [(sb(K, st, "atm%d" % i, [128, 4, 128], BF16), Buf()) for i in range(2)]
        ost = [(sb(K, st, "ost%d" % i, [128, 8, 128], F32), Buf()) for i in range(2)]
        sqt = sb(K, st, "gsq", [128, 8, 128], F32)
        sqb = Buf()
        srt = [(sb(K, st, "srt%d" % i, [128, 8, 128], F32), Buf()) for i in range(1)]
        rst = sb(K, st, "grst", [128, 512], F32)
        rstb = Buf()
        tmp = sb(K, st, "gtmp", [128, 128], F32)
        tmpb = Buf()
        yst = [(sb(K, st, "gyst%d" % i, [128, 8, 128], BF16), Buf()) for i in range(1)]
        nt_ = 0
        for si, (t0, L, smp) in enumerate(SEQS):
            ntl = L // 128
            K.qs.dma(vv[:, 0:ntl, :], d["gv_tok"][t0:t0 + L, :].rearrange("(j p) c -> p j c", p=128), writes=[vb])
            for dd in range(2):
                pst = {}

                def prepA(t):
                    global_idx = pst.setdefault("n", 0)
                    pst["n"] = global_idx + 1
                    tg = t0 + t * 128
                    tl = slice(t * 128, (t + 1) * 128)
                    qf, qfb = qfs[global_idx % 2]
                    zs, zb = zss[global_idx % 2]
                    cs_t, csb = css[global_idx % 2]
                    lg, lgb = lgs[global_idx % 2]
                    rt, rtb = rts[global_idx % 2]
                    kd, kdbf = kds[global_idx % 2]
                    pst[t] = (qf, qfb, lg, lgb, kd, kdbf)
                    K.qs.dma(zs[:], d["gzT"][:, :, tg:tg + 128].rearrange("d r t -> r d t"), writes=[zb])
                    if smp:
                        K.qs.dma(cs_t[:, 0, :], d["rope_cos"][:, tl], writes=[csb])
                        K.qs.dma(cs_t[:, 1, :], d["rope_sin"][:, tl], writes=[csb])
                    srcs = ("gqT", "gkT", "gqpT", "gkpT") if smp else ("gqT", "gkT")
                    for i, nm in enumerate(srcs):
                        K.qs.dma(qf[:, i, :, :], d[nm][:, :, tg:tg + 128].rearrange("h p t -> p h t"), writes=[qfb])
                    pt, pb = ps_alloc(K)
                    K.pe.sync([zb, cb], [pb])
                    ins = nc.tensor.matmul(pt[:], lhsT=zs[:, dd, :], rhs=wg[:, dd, :], start=True, stop=True)
                    K.pe.done(ins, [zb, cb], [pb])
                    K.dve.op(lambda e: e.tensor_tensor(out=lg[:], in0=pt[:], in1=bgb[:, dd, :], op=ALU.add), reads=[pb, cb], writes=[lgb])
                    K.act.op(lambda e: e.activation(out=lg[:], in_=lg[:], func=AF.Sigmoid), reads=[lgb], writes=[lgb])
                    K.act.op(lambda e: e.activation(out=lg[:], in_=lg[:], func=AF.Ln), reads=[lgb], writes=[lgb])
                    if smp:
                        for i in range(2):
                            K.dve.op(lambda e: e.tensor_tensor(out=qf[:, i, :, :], in0=qf[:, i, :, :], in1=cs_t[:, 0, :].unsqueeze(1).to_broadcast([128, 4, 128]), op=ALU.mult),
                                     reads=[qfb, csb], writes=[qfb])
                            K.dve.op(lambda e: e.tensor_tensor(out=rt[:], in0=qf[:, 2 + i, :, :], in1=cs_t[:, 1, :].unsqueeze(1).to_broadcast([128, 4, 128]), op=ALU.mult),
                                     reads=[qfb, csb], writes=[rtb])
                            K.dve.op(lambda e: e.tensor_tensor(out=qf[:, i, :, :], in0=qf[:, i, :, :], in1=rt[:], op=ALU.add), reads=[qfb, rtb], writes=[qfb])

                def prepB(t):
                    tl = slice(t * 128, (t + 1) * 128)
                    qf, qfb, lg, lgb, kd, kdbf = pst.pop(t)
                    for hp in range(2):
                        bt, btb = ps_alloc(K)
                        K.pe.sync([lgb, cb], [btb])
                        for hl in range(2):
                            h = hp * 2 + hl
                            ins = nc.tensor.matmul(bt[:, hl * 256:(hl + 1) * 256], lhsT=lg[:, h * 128:(h + 1) * 128], rhs=tri[:, dd, :], start=True, stop=True)
                        K.pe.done(ins, [lgb, cb], [btb])
                        for hl in range(2):
                            h = hp * 2 + hl
                            e1, e1b = e1s[h % 2]
                            e2, e2b = e2s[h % 2]
                            K.act.op(lambda e: e.activation(out=e1[:], in_=bt[:, hl * 256:(hl + 1) * 256], func=AF.Exp), reads=[btb], writes=[e1b])
                            K.act.op(lambda e: e.activation(out=e2[:], in_=bt[:, hl * 256:hl * 256 + 128], func=AF.Exp, scale=-1.0), reads=[btb], writes=[e2b])
                            K.dve.op(lambda e: e.scalar_tensor_tensor(out=qin[:, h, tl], in0=qf[:, 0, h, :], scalar=SCALE, in1=e1[:, 0:128], op0=ALU.mult, op1=ALU.mult),
                                     reads=[qfb, e1b], writes=[qb_])
                            K.dve.op(lambda e: e.tensor_tensor(out=kout[:, h, tl], in0=qf[:, 1, h, :], in1=e2[:], op=ALU.mult), reads=[qfb, e2b], writes=[kb_])
                            K.dve.op(lambda e: e.tensor_tensor(out=kd[:, h, :], in0=qf[:, 1, h, :], in1=e1[:, 128:256], op=ALU.mult), reads=[qfb, e1b], writes=[kdbf])
                            col = 63 if dd == 0 else 0
                            K.dve.op(lambda e: e.tensor_copy(out=dec[:, h, 2 * t:2 * t + 2], in_=e1[:, 0:128].rearrange("p (c i) -> p c i", c=2)[:, :, col]),
                                     reads=[e1b], writes=[decb])
                    tp, tpb = ps_alloc(K)
                    K.pe.sync([kdbf, K.cbuf], [tpb])
                    for h in range(4):
                        ins = nc.tensor.transpose(tp[:, h * 128:(h + 1) * 128], kd[:, h, :], K.ident[:])
                    K.pe.done(ins, [kdbf, K.cbuf], [tpb])
                    K.act.op(lambda e: e.activation(out=kdt[:, t, :, :], in_=tp[:].rearrange("p (h e) -> p h e", h=4), func=AF.Copy), reads=[tpb], writes=[kdb])

                for i in range(ntl + 1):
                    if i < ntl:
                        prepA(i)
                    if i >= 1:
                        prepB(i - 1)
                if smp:
                    K.qs.dma(Sf[:], d["sg_in"][l, dd].rearrange("h k e -> k h e"), writes=Sfb)
                else:
                    K.dve.op(lambda e: e.memset(Sf[:], 0.0), writes=Sfb)
                K.dve.op(lambda e: e.tensor_copy(out=Sb[:], in_=Sf[:]), reads=Sfb, writes=Sbb)
                order = list(range(ntl)) if dd == 0 else list(range(ntl - 1, -1, -1))
                for oi_, t in enumerate(order):
                    tg = t0 + t * 128
                    tl = slice(t * 128, (t + 1) * 128)
                    at_, atb = K.ps[oi_ % 2]
                    am, amb = atm[oi_ % 2]
                    K.pe.sync([qb_, kb_], [atb])
                    for h in range(4):
                        ins = nc.tensor.matmul(at_[:, h * 128:(h + 1) * 128], lhsT=kout[:, h, tl], rhs=qin[:, h, tl], start=True, stop=True)
                    K.pe.done(ins, [qb_, kb_], [atb])
                    K.dve.op(lambda e: e.tensor_tensor(out=am[:], in0=at_[:].rearrange("p (h c) -> p h c", h=4), in1=msk[:, dd, :].unsqueeze(1).to_broadcast([128, 4, 128]), op=ALU.mult),
                             reads=[atb, cb], writes=[amb])
                    obanks = [K.ps[2], K.ps[3]]
                    sbanks = [K.ps[4], K.ps[5]]
                    corder = (0, 1) if dd == 0 else (1, 0)
                    for ci_, c in enumerate(corder):
                        cs = slice(t * 128 + c * 64, t * 128 + (c + 1) * 64)
                        prt = slice(c * 64, (c + 1) * 64)
                        for h in range(4):
                            ob, obb = obanks[h // 2]
                            sbk, sbb = sbanks[h // 2]
                            base = (h % 2) * 256
                            K.pe.sync([amb, vb, qb_, Sbb[h]], [obb] if ci_ == 0 else [])
                            for eh in range(2):
                                reg = slice(base + eh * 128 + c * 64, base + eh * 128 + (c + 1) * 64)
                                nc.tensor.matmul(ob[:, reg], lhsT=vv[:, t, h * 256 + eh * 128:h * 256 + (eh + 1) * 128], rhs=am[:, h, c * 64:(c + 1) * 64],
                                                 start=True, stop=False, skip_group_check=True)
                                ins = nc.tensor.matmul(ob[:, reg], lhsT=Sb[:, h, eh * 128:(eh + 1) * 128], rhs=qin[:, h, cs],
                                                       start=False, stop=True, skip_group_check=True)
                            K.pe.done(ins, [amb, vb, qb_, Sbb[h]], [obb])
                            K.pe.sync([kdb, vb], [sbb])
                            ins = nc.tensor.matmul(sbk[:, base:base + 256], lhsT=kdt[prt, t, h, :], rhs=vv[prt, t, h * 256:(h + 1) * 256], start=True, stop=True)
                            K.pe.done(ins, [kdb, vb], [sbb])
                        for h in range(4):
                            sbk, sbb = sbanks[h // 2]
                            base = (h % 2) * 256
                            K.dve.op(lambda e: e.scalar_tensor_tensor(out=Sb[:, h, :], in0=Sf[:, h, :], scalar=dec[:, h, 2 * t + c:2 * t + c + 1], in1=sbk[:, base:base + 256],
                                                                      op0=ALU.mult, op1=ALU.add), reads=[Sfb[h], decb, sbb], writes=[Sbb[h]])
                            K.dve.op(lambda e: e.scalar_tensor_tensor(out=Sf[:, h, :], in0=Sf[:, h, :], scalar=dec[:, h, 2 * t + c:2 * t + c + 1], in1=sbk[:, base:base + 256],
                                                                      op0=ALU.mult, op1=ALU.add), reads=[Sfb[h], decb, sbb], writes=[Sfb[h]])
                    os_, osb = ost[oi_ % 2]
                    if dd == 0:
                        for hp in range(2):
                            ob, obb = obanks[hp]
                            K.act.op(lambda e: e.activation(out=os_[:, hp * 4:(hp + 1) * 4, :], in_=ob[:].rearrange("p (c t) -> p c t", c=4), func=AF.Copy), reads=[obb], writes=[osb])
                        K.qs.dma(d["ogT"][:, :, tg:tg + 128].rearrange("c p t -> p c t"), os_[:], reads=[osb])
                    else:
                        K.qs.dma(os_[:], d["ogT"][:, :, tg:tg + 128].rearrange("c p t -> p c t"), writes=[osb])
                        sr_, srb = srt[0]
                        K.qs.dma(sr_[:], d["srT"][:, :, tg:tg + 128].rearrange("c p t -> p c t"), writes=[srb])
                        for hp in range(2):
                            ob, obb = obanks[hp]
                            K.dve.op(lambda e: e.tensor_tensor(out=os_[:, hp * 4:(hp + 1) * 4, :], in0=os_[:, hp * 4:(hp + 1) * 4, :], in1=ob[:].rearrange("p (c t) -> p c t", c=4), op=ALU.add),
                                     reads=[obb, osb], writes=[osb])
                        K.act.op(lambda e: e.activation(out=sqt[:], in_=os_[:], func=AF.Square), reads=[osb], writes=[sqb])
                        stt, stb = K.ps[6]
                        K.pe.sync([sqb, K.cbuf], [stb])
                        for h in range(4):
                            for eh in range(2):
                                ins = nc.tensor.matmul(stt[:, h * 128:(h + 1) * 128], lhsT=K.ones[:], rhs=sqt[:, 2 * h + eh, :], start=(eh == 0), stop=(eh == 1))
                        K.pe.done(ins, [sqb, K.cbuf], [stb])
                        rstd_from_stats(K, rst[:], stt[:], [stb], [rstb], dim=256)
                        ys, ysb = yst[0]
                        for c8 in range(8):
                            K.dve.op(lambda e: e.tensor_tensor(out=tmp[:], in0=os_[:, c8, :], in1=rst[:, (c8 // 2) * 128:(c8 // 2 + 1) * 128], op=ALU.mult), reads=[osb, rstb], writes=[tmpb])
                            K.dve.op(lambda e: e.scalar_tensor_tensor(out=ys[:, c8, :], in0=tmp[:], scalar=gn[:, c8:c8 + 1], in1=sr_[:, c8, :], op0=ALU.mult, op1=ALU.mult),
                                     reads=[tmpb, cb, srb], writes=[ysb])
                        K.qs.dma(d["brT"][16:24, :, tg:tg + 128].rearrange("c p t -> p c t"), ys[:], reads=[ysb])
                if not smp:
                    K.qs.dma(d["new_s"][si, l, dd].rearrange("h k e -> k h e"), Sf[:], reads=Sfb)


def mix_merge(K, l):
    nc = K.nc
    d = K.dram
    K.din("w_branch", [DEPTH, 3, 1024, D])
    K.din("w_out", [DEPTH, D, D])
    wbr = d["w_branch"][l]
    wo = d["w_out"][l]
    sig4 = d["sigT"].rearrange("(n c) p t -> c p n t", n=3)
    with ExitStack() as st:
        tiles = gemm2_tiles(K, st)
        sgs = [(sb(K, st, "msg%d" % i, [128, 3, 512], F32), Buf()) for i in range(2)]
        m0s = [(sb(K, st, "mm%d" % i, [128, 512], F32), Buf()) for i in range(2)]
        m1 = sb(K, st, "mm1", [128, 512], F32)
        m1b = Buf()
        cnt = [0]
        for (t0, blk) in ((0, 1024), (1024, 1024), (2048, 512)):
            nsb = blk // 512
            br = K.big[:, 0:24 * blk].rearrange("p (k t) -> p k t", k=24)
            mg = K.big[:, 24 * blk:40 * blk].rearrange("p (k t) -> p k t", k=16)
            brb = K.bigbuf[0:3]
            mgb = K.bigbuf[3]
            for n in range(3):
                K.qs.dma(br[:, n * 8:(n + 1) * 8, :], d["brT"][n * 8:(n + 1) * 8, :, t0:t0 + blk].rearrange("k p t -> p k t"), writes=[brb[n]])
            jobs = [[wbr[n][:, dc * 128:(dc + 1) * 128] for n in range(3)] for dc in range(NFC)]

            def epi(j, s, banks):
                sg, sgb = sgs[cnt[0] % 2]
                m0, m0b = m0s[cnt[0] % 2]
                cnt[0] += 1
                K.qs.dma(sg[:], sig4[j][:, :, t0 + s * 512:t0 + (s + 1) * 512], writes=[sgb])
                K.dve.op(lambda e: e.tensor_tensor(out=m0[:], in0=banks[0][0][:], in1=sg[:, 0, :], op=ALU.mult), reads=[banks[0][1], sgb], writes=[m0b])
                K.dve.op(lambda e: e.tensor_tensor(out=m1[:], in0=banks[1][0][:], in1=sg[:, 1, :], op=ALU.mult), reads=[banks[1][1], sgb], writes=[m1b])
                K.dve.op(lambda e: e.tensor_tensor(out=m0[:], in0=m0[:], in1=m1[:], op=ALU.add), reads=[m0b, m1b], writes=[m0b])
                K.dve.op(lambda e: e.tensor_tensor(out=m1[:], in0=banks[2][0][:], in1=sg[:, 2, :], op=ALU.mult), reads=[banks[2][1], sgb], writes=[m1b])
                K.dve.op(lambda e: e.tensor_tensor(out=mg[:, j, s * 512:(s + 1) * 512], in0=m0[:], in1=m1[:], op=ALU.add), reads=[m0b, m1b], writes=[mgb])
            gemm_b(K, jobs, 8, nsb, 512, lambda k, s, i: br[:, i * 8 + k, s * 512:(s + 1) * 512], lambda s: brb, epi)
            gemm2_core(K, tiles, [[wo[:, fo * 128:(fo + 1) * 128]] for fo in range(NFC)], NFC, t0, blk,
                       lambda k, s, i: mg[:, k, s * 512:(s + 1) * 512], lambda s: [mgb])
            barrier(K)


def stage_in(K):
    nc = K.nc
    with ExitStack() as st:
        xin = [(sb(K, st, "xin%d" % i, [128, D], F32), Buf()) for i in range(3)]
        xo = [(sb(K, st, "xo%d" % i, [128, NFC, 128], F32), [Buf() for _ in range(4)]) for i in range(3)]
        for t in range(NT):
            xt, xb = xin[t % 3]
            ot, obs = xo[t % 3]
            K.qs.dma(xt[:], K.dram["x_in"][t * 128:(t + 1) * 128, :], writes=[xb])
            for g in range(4):
                pt, pb = K.ps[(t * 4 + g) % 8]
                K.pe.sync([xb, K.cbuf], [pb])
                for c in range(4):
                    fc = g * 4 + c
                    ins = nc.tensor.transpose(pt[:, c * 128:(c + 1) * 128], xt[:, fc * 128:(fc + 1) * 128], K.ident[:])
                K.pe.done(ins, [xb, K.cbuf], [pb])
                eng = K.act if g % 2 == 0 else K.dve
                if eng is K.act:
                    eng.op(lambda e: e.activation(out=ot[:, g * 4:(g + 1) * 4, :], in_=pt[:].rearrange("p (c t) -> p c t", c=4), func=AF.Copy), reads=[pb], writes=[obs[g]])
                else:
                    eng.op(lambda e: e.tensor_copy(out=ot[:, g * 4:(g + 1) * 4, :], in_=pt[:].rearrange("p (c t) -> p c t", c=4)), reads=[pb], writes=[obs[g]])
            K.qs.dma(K.dram["xT"][:, :, t * 128:(t + 1) * 128].rearrange("c p t -> p c t"), ot[:], reads=obs)


def stage_out(K):
    nc = K.nc
    with ExitStack() as st:
        xin = [(sb(K, st, "xfin%d" % i, [128, NFC, 128], F32), Buf()) for i in range(3)]
        xo = [(sb(K, st, "xfo%d" % i, [128, D], F32), [Buf() for _ in range(4)]) for i in range(3)]
        for t in range(NT):
            xt, xb = xin[t % 3]
            ot, obs = xo[t % 3]
            K.qs.dma(xt[:], K.dram["xT"][:, :, t * 128:(t + 1) * 128].rearrange("c p t -> p c t"), writes=[xb])
            for g in range(4):
                pt, pb = K.ps[(t * 4 + g) % 8]
                K.pe.sync([xb, K.cbuf], [pb])
                for c in range(4):
                    fc = g * 4 + c
                    ins = nc.tensor.transpose(pt[:, c * 128:(c + 1) * 128], xt[:, fc, :], K.ident[:])
                K.pe.done(ins, [xb, K.cbuf], [pb])
                eng = K.act if g % 2 == 0 else K.dve
                if eng is K.act:
                    eng.op(lambda e: e.copy(out=ot[:, g * 512:(g + 1) * 512], in_=pt[:]), reads=[pb], writes=[obs[g]])
                else:
                    eng.op(lambda e: e.tensor_copy(out=ot[:, g * 512:(g + 1) * 512], in_=pt[:]), reads=[pb], writes=[obs[g]])
            K.qs.dma(K.dram["y_out"][t * 128:(t + 1) * 128, :], ot[:], reads=obs)


def _consts():
    c = {}
    c["ident_in"] = np.eye(128, dtype=np.float32)
    pm = np.zeros((128, 20, 128), np.float32)
    L = 512
    t = np.arange(L)
    for g, win in enumerate((2, 4, 8, 16)):
        lo = np.clip(t - win // 2, 0, L - 1)
        hi = np.clip(t + win - 1 - win // 2, 0, L - 1)
        cnt = (hi - lo + 1).astype(np.float32)
        P = ((t[:, None] >= lo[None, :]) & (t[:, None] <= hi[None, :])).astype(np.float32) / cnt[None, :]
        P = P - np.eye(L, dtype=np.float32)
        pm[:, g * 5 + 0, :] = P[128:256, 128:256]
        pm[:, g * 5 + 1, :] = P[0:128, 0:128]
        pm[:, g * 5 + 2, :] = P[384:512, 384:512]
        pm[:, g * 5 + 3, :] = P[0:128, 128:256]
        pm[:, g * 5 + 4, :] = P[256:384, 128:256]
    c["pool_pm"] = pm.astype(ml_dtypes.bfloat16)
    s_ = np.arange(128)[:, None]
    t_ = np.arange(128)[None, :]
    same = (s_ // 64) == (t_ // 64)
    tri = np.zeros((128, 2, 256), np.float32)
    tri[:, 0, 0:128] = (same & (s_ <= t_)) / 16.0
    tri[:, 0, 128:256] = (same & (s_ > t_)) / 16.0
    tri[:, 1, 0:128] = (same & (s_ >= t_)) / 16.0
    tri[:, 1, 128:256] = (same & (s_ < t_)) / 16.0
    c["gla_tri"] = tri
    msk = np.zeros((128, 2, 128), np.float32)
    msk[:, 0, :] = same & (s_ <= t_)
    msk[:, 1, :] = same & (s_ >= t_)
    c["gla_mask"] = msk
    tt = np.arange(TS)
    pos = (tt // GRID_W, tt % GRID_W)
    inv = (np.float32(10000.0) ** (-np.arange(32, dtype=np.float32) / np.float32(32))).astype(np.float32)
    cos = np.zeros((128, TS), np.float32)
    sin = np.zeros((128, TS), np.float32)
    for ax in range(2):
        ang = pos[ax].astype(np.float32)[None, :] * inv[:, None]
        cs_, sn_ = np.cos(ang).astype(np.float32), np.sin(ang).astype(np.float32)
        b = ax * 64
        cos[b:b + 32] = cs_
        cos[b + 32:b + 64] = cs_
        sin[b:b + 32] = -sn_
        sin[b + 32:b + 64] = sn_
    c["rope_cos"] = cos
    c["rope_sin"] = sin
    return c


def _rope_cols():
    cols = []
    for base in (C_GQ, C_GK):
        for h in range(4):
            for j in range(128):
                jl = j % 64
                partner = j + 32 if jl < 32 else j - 32
                cols.append(base + h * 128 + partner)
    return np.array(cols)


def _na_tt(rpb):
    c = np.arange(64)[:, None]
    kc = np.arange(64)[None, :]
    cs = np.clip(c - 8, 0, 48)
    ok = (kc >= cs) & (kc < cs + 16)
    idx = np.clip(kc - c + 15, 0, 30)
    g = rpb[:, :, :, idx]
    g = np.where(ok[None, None, None], g, np.float32(-1e30))
    return np.ascontiguousarray(g.transpose(0, 3, 1, 2, 4)).astype(np.float32)


def make_in_maps(inputs, cores):
    consts = _consts()
    shared = dict(consts)
    for k in ("w_mod", "b_mod", "norm_pre", "norm_post", "w_ffn_in", "w_ffn_out", "w_in", "pool_w", "pool_scale",
              "gla_w_gate", "gla_b_gate", "gla_norm", "w_branch", "w_out"):
        shared[k] = np.ascontiguousarray(inputs[k])
    shared["w_rope"] = np.ascontiguousarray(inputs["w_in"][:, :, _rope_cols()])
    shared["na_tt"] = _na_tt(np.asarray(inputs["na_rpb"]))
    maps = []
    for c in cores:
        s = c % 4
        m = dict(shared)
        m["x_in"] = np.ascontiguousarray(np.concatenate([inputs["x_prompt"][2 * c:2 * c + 2].reshape(TP, D), inputs["x_sample"][s]], 0))
        m["cvec"] = np.ascontiguousarray(np.stack([inputs["c_ctx"], inputs["c"][s]]))
        m["ck_in"] = np.ascontiguousarray(inputs["cache_na_k"][s])
        m["cv_in"] = np.ascontiguousarray(inputs["cache_na_v"][s])
        m["sg_in"] = np.ascontiguousarray(inputs["state_gla"][s])
        maps.append(m)
    return maps


def run(inputs, cores, dbg=None, trace=False):
    nc = build(dbg)
    maps = make_in_maps(inputs, cores)
    names = set(nc_input_names(nc))
    maps = [{k: v for k, v in m.items() if k in names} for m in maps]
    return run_bass_kernel_spmd(nc, maps, core_ids=list(range(len(cores))), trace=trace)


_INPUT_NAMES = []


def nc_input_names(nc):
    return list(_INPUT_NAMES)


def kernel(**inputs):
    inputs = {k: np.asarray(v) for k, v in inputs.items()}
    res = run(inputs, list(range(8)))
    r = res.results
    y_prompt = np.concatenate([r[c]["y_out"][:TP].reshape(2, 256, D) for c in range(8)], 0)
    y_sample = np.stack([r[s]["y_out"][TP:] for s in range(4)], 0)
    if "new_k" in r[0]:
        new_k = np.concatenate([r[c]["new_k"] for c in range(8)], 0)
        new_v = np.concatenate([r[c]["new_v"] for c in range(8)], 0)
        new_s = np.concatenate([r[c]["new_s"] for c in range(8)], 0)
    else:
        new_k = np.zeros((16, DEPTH, 8, 256, 128), np.float32)
        new_v = np.zeros((16, DEPTH, 8, 256, 128), np.float32)
        new_s = np.zeros((16, DEPTH, 2, 4, 128, 256), np.float32)
    return (y_prompt.astype(np.float32), y_sample.astype(np.float32), new_k.astype(np.float32),
            new_v.astype(np.float32), new_s.astype(np.float32))
```
